# Optimizing a Trainium2 kernel written in Bass

```python
import jax, jax.numpy as jnp
from jax import lax
import numpy as np

D_MODEL = 1024
BATCH = 4
SEQ = 8192
DEPTH = 4

W_MIX = 1024
N_BRANCH = 3
W_A = W_MIX
H_A = 4
DH_A = W_A // H_A
MLSTM_CHUNK = 64
QK_CONV = 4
W_B = W_MIX
CONF_K = 31
W_C = W_MIX
H_C = 8
DH_C = W_C // H_C
G_C = 2
HPG_C = H_C // G_C
KV_W = G_C * DH_C
CMP_LEN = 32
CMP_STRIDE = 16
CMP_HIDDEN = 256
SLC_LEN = 64
N_SLC = 16
WINDOW = 512
NSA_QBLOCK = 64
NEG = -1e30

IN_SPLITS = (W_A, W_A, W_A, W_A, W_A, H_A, H_A,
             W_B, W_B, W_B,
             W_C, KV_W, KV_W, KV_W, KV_W, KV_W, KV_W, 3 * H_C, W_C,
             N_BRANCH * D_MODEL)
N_IN = sum(IN_SPLITS)
F_GATE_OFFSET = 5 * W_A + H_A

kernel_name = 'hybrid_mlstm_conformer_nsa_trunk'


def split_cols(t):
    return jnp.split(t, np.cumsum(IN_SPLITS)[:-1].tolist(), axis=-1)


def rmsnorm(x, g, eps=1e-6):
    xf = x.astype(jnp.float32)
    y = xf * lax.rsqrt(jnp.mean(xf * xf, axis=-1, keepdims=True) + eps)
    return y.astype(x.dtype) * g


def layernorm(x, g, b, eps=1e-5):
    xf = x.astype(jnp.float32)
    mu = jnp.mean(xf, axis=-1, keepdims=True)
    var = jnp.mean(jnp.square(xf - mu), axis=-1, keepdims=True)
    y = (xf - mu) * lax.rsqrt(var + eps)
    return y.astype(x.dtype) * g + b


def causal_dwconv(x, w, b):
    k = w.shape[0]
    y = lax.conv_general_dilated(x, w[:, None, :].astype(x.dtype), (1,), [(k - 1, 0)],
                                 dimension_numbers=('NWC', 'WIO', 'NWC'),
                                 feature_group_count=x.shape[-1])
    return y + b


def masked_softmax(s, mask):
    s = jnp.where(mask, s, NEG)
    p = jnp.exp(s - jnp.max(s, axis=-1, keepdims=True)) * mask
    return p / jnp.maximum(jnp.sum(p, axis=-1, keepdims=True), 1e-30)


def mlstm_chunkwise(q, k, v, ig, lf):
    b, s, h, dk = q.shape
    dv = v.shape[-1]
    nc = s // MLSTM_CHUNK

    def chunks(t):
        t = t.astype(jnp.float32).reshape((b, nc, MLSTM_CHUNK, h) + t.shape[3:])
        return jnp.moveaxis(jnp.moveaxis(t, 1, 0), 3, 2)

    causal = jnp.tril(jnp.ones((MLSTM_CHUNK, MLSTM_CHUNK), bool))

    def step(carry, inp):
        C, n, m = carry
        qc, kc, vc, ic, fc = inp
        bcum = jnp.cumsum(fc, axis=-1)
        dmat = jnp.where(causal, bcum[..., :, None] - bcum[..., None, :] + ic[..., None, :], -jnp.inf)
        m_inter = bcum + m[..., None]
        m_t = jnp.maximum(m_inter, jnp.max(dmat, axis=-1))
        w = jnp.exp(dmat - m_t[..., None])
        a_inter = jnp.exp(m_inter - m_t)
        sqk = jnp.einsum('bhtd,bhsd->bhts', qc, kc) * w
        num = jnp.einsum('bhts,bhse->bhte', sqk, vc) + a_inter[..., None] * jnp.einsum('bhtd,bhde->bhte', qc, C)
        den = jnp.sum(sqk, axis=-1) + a_inter * jnp.einsum('bhtd,bhd->bht', qc, n)
        h_out = num / jnp.maximum(jnp.abs(den), jnp.exp(-m_t))[..., None]
        b_last = bcum[..., -1]
        g = b_last[..., None] - bcum + ic
        m_new = jnp.maximum(b_last + m, jnp.max(g, axis=-1))
        decay = jnp.exp(b_last + m - m_new)
        wk = jnp.exp(g - m_new[..., None])[..., None] * kc
        C = decay[..., None, None] * C + jnp.einsum('bhsd,bhse->bhde', wk, vc)
        n = decay[..., None] * n + jnp.sum(wk, axis=2)
        return (C, n, m_new), h_out

    init = (jnp.zeros((b, h, dk, dv), jnp.float32), jnp.zeros((b, h, dk), jnp.float32),
            jnp.zeros((b, h), jnp.float32))
    _, hs = lax.scan(step, init, (chunks(q), chunks(k), chunks(v), chunks(ig), chunks(lf)))
    return hs.transpose(1, 0, 3, 2, 4).reshape(b, s, h, dv).astype(q.dtype)


def nsa(q, k_cmp, v_cmp, k_slc, v_slc, k_win, v_win, gates, pe, w1, w2):
    b, s = q.shape[:2]
    scale = DH_C ** -0.5

    def compress(t, j):
        tb = t.reshape(b, s // CMP_STRIDE, CMP_STRIDE, G_C, DH_C)
        blk = jnp.concatenate([tb[:, :-1], tb[:, 1:]], axis=2) + pe[j][None, None, :, None, :]
        blk = blk.transpose(0, 1, 3, 2, 4).reshape(b, -1, G_C, CMP_LEN * DH_C)
        return jax.nn.gelu(blk @ w1[j]) @ w2[j]

    kc = compress(k_cmp, 0)
    vc = compress(v_cmp, 1)
    n_cmp = kc.shape[1]
    cmp_start = jnp.arange(n_cmp) * CMP_STRIDE
    cmp_end = cmp_start + CMP_LEN - 1
    n_slc = s // SLC_LEN
    n_sel = min(N_SLC, n_slc)
    slc_start = jnp.arange(n_slc) * SLC_LEN
    overlap = ((cmp_start[:, None] < slc_start[None, :] + SLC_LEN) &
               (cmp_start[:, None] + CMP_LEN > slc_start[None, :])).astype(jnp.float32)
    ks_blocks = k_slc.reshape(b, n_slc, SLC_LEN, G_C, DH_C).transpose(0, 3, 1, 2, 4)
    vs_blocks = v_slc.reshape(b, n_slc, SLC_LEN, G_C, DH_C).transpose(0, 3, 1, 2, 4)
    pad = ((0, 0), (WINDOW, 0), (0, 0), (0, 0))
    kw_pad = jnp.pad(k_win, pad)
    vw_pad = jnp.pad(v_win, pad)
    gather = jax.vmap(jax.vmap(lambda blocks, idx: blocks[idx]))
    jb = jnp.arange(n_slc)

    def block(i):
        qs = i * NSA_QBLOCK
        t = qs + jnp.arange(NSA_QBLOCK)
        qb = lax.dynamic_slice_in_dim(q, qs, NSA_QBLOCK, axis=1).reshape(b, NSA_QBLOCK, G_C, HPG_C, DH_C)
        gb = jax.nn.sigmoid(lax.dynamic_slice_in_dim(gates, qs, NSA_QBLOCK, axis=1)
                            .reshape(b, NSA_QBLOCK, G_C, HPG_C, 3))
        s_c = jnp.einsum('bqgjd,bngd->bgjqn', qb, kc).astype(jnp.float32) * scale
        p_c = masked_softmax(s_c, cmp_end[None, :] <= t[:, None])
        o_c = jnp.einsum('bgjqn,bngd->bqgjd', p_c.astype(vc.dtype), vc)
        imp = jnp.einsum('bgjqn,nm->bgqm', p_c, overlap)
        cur = t // SLC_LEN
        forced = (jb[None] == 0) | (jb[None] == cur[:, None]) | (jb[None] == cur[:, None] - 1)
        imp = jnp.where(jb[None] > cur[:, None], -1e4, jnp.where(forced, 1e4, imp))
        _, idx = lax.top_k(imp, n_sel)
        ks = gather(ks_blocks, idx).reshape(b, G_C, NSA_QBLOCK, n_sel * SLC_LEN, DH_C)
        vs = gather(vs_blocks, idx).reshape(b, G_C, NSA_QBLOCK, n_sel * SLC_LEN, DH_C)
        pos = (idx[..., None] * SLC_LEN + jnp.arange(SLC_LEN)).reshape(b, G_C, NSA_QBLOCK, n_sel * SLC_LEN)
        s_s = jnp.einsum('bqgjd,bgqkd->bgjqk', qb, ks).astype(jnp.float32) * scale
        p_s = masked_softmax(s_s, (pos <= t[:, None])[:, :, None])
        o_s = jnp.einsum('bgjqk,bgqkd->bqgjd', p_s.astype(vs.dtype), vs)
        kw = lax.dynamic_slice_in_dim(kw_pad, qs, NSA_QBLOCK + WINDOW, axis=1)
        vw = lax.dynamic_slice_in_dim(vw_pad, qs, NSA_QBLOCK + WINDOW, axis=1)
        pw = qs - WINDOW + jnp.arange(NSA_QBLOCK + WINDOW)
        dlt = t[:, None] - pw[None, :]
        mask_w = (dlt >= 0) & (dlt < WINDOW) & (pw[None, :] >= 0)
        s_w = jnp.einsum('bqgjd,bkgd->bgjqk', qb, kw).astype(jnp.float32) * scale
        p_w = masked_softmax(s_w, mask_w)
        o_w = jnp.einsum('bgjqk,bkgd->bqgjd', p_w.astype(vw.dtype), vw)
        return gb[..., 0:1] * o_c + gb[..., 1:2] * o_s + gb[..., 2:3] * o_w

    out = lax.map(block, jnp.arange(s // NSA_QBLOCK))
    return out.transpose(1, 0, 2, 3, 4, 5).reshape(b, s, W_C)


def setup_inputs(seed: int = 0) -> dict:
    key = jax.random.key(seed)
    ks = jax.random.split(key, 20)
    L = DEPTH

    def nrm(k, shape, sc):
        return jax.random.normal(k, shape, jnp.float32) * sc

    b_in = nrm(ks[7], (L, N_IN), 0.02)
    b_in = b_in.at[:, F_GATE_OFFSET:F_GATE_OFFSET + H_A].add(jnp.linspace(3.0, 6.0, H_A))
    return {
        'x': nrm(ks[0], (BATCH, SEQ, D_MODEL), 1.0),
        'c': nrm(ks[1], (BATCH, D_MODEL), 1.0),
        'w_ada': nrm(ks[2], (L, D_MODEL, 3 * D_MODEL), 0.3 * D_MODEL ** -0.5),
        'b_ada': nrm(ks[3], (L, 3 * D_MODEL), 0.02),
        'norm_pre': 1.0 + nrm(ks[4], (L, D_MODEL), 0.02),
        'norm_post': 1.0 + nrm(ks[5], (L, D_MODEL), 0.02),
        'w_in': nrm(ks[6], (L, D_MODEL, N_IN), D_MODEL ** -0.5),
        'b_in': b_in,
        'mlstm_conv_w': nrm(ks[8], (L, QK_CONV, 2 * W_A), QK_CONV ** -0.5),
        'mlstm_conv_b': nrm(ks[9], (L, 2 * W_A), 0.02),
        'mlstm_norm': 1.0 + nrm(ks[10], (L, W_A), 0.02),
        'conf_dw_w': nrm(ks[11], (L, CONF_K, W_B), CONF_K ** -0.5),
        'conf_dw_b': nrm(ks[12], (L, W_B), 0.02),
        'conf_ln_g': 1.0 + nrm(ks[13], (L, W_B), 0.02),
        'conf_ln_b': nrm(ks[14], (L, W_B), 0.02),
        'nsa_cmp_pe': nrm(ks[15], (L, 2, CMP_LEN, DH_C), 0.02),
        'nsa_cmp_w1': nrm(ks[16], (L, 2, CMP_LEN * DH_C, CMP_HIDDEN), (CMP_LEN * DH_C) ** -0.5),
        'nsa_cmp_w2': nrm(ks[17], (L, 2, CMP_HIDDEN, DH_C), CMP_HIDDEN ** -0.5),
        'w_branch': nrm(ks[18], (L, N_BRANCH, W_MIX, D_MODEL), W_MIX ** -0.5),
        'w_out': nrm(ks[19], (L, D_MODEL, D_MODEL), D_MODEL ** -0.5),
    }


def reference(x, c, w_ada, b_ada, norm_pre, norm_post, w_in, b_in, mlstm_conv_w, mlstm_conv_b,
              mlstm_norm, conf_dw_w, conf_dw_b, conf_ln_g, conf_ln_b, nsa_cmp_pe, nsa_cmp_w1,
              nsa_cmp_w2, w_branch, w_out):
    b, s = x.shape[:2]
    for l in range(DEPTH):
        shift, scale, gate = jnp.split(c @ w_ada[l] + b_ada[l], 3, axis=-1)
        h = rmsnorm(x, norm_pre[l]) * (1.0 + scale[:, None]) + shift[:, None]
        proj = h @ w_in[l] + b_in[l]
        (aq, ak, av, ao, az, ai, af, ba, bb, bz,
         cq, ckc, cvc, cks, cvs, ckw, cvw, cg, cz, mg) = split_cols(proj)

        qk = jax.nn.silu(causal_dwconv(jnp.concatenate([aq, ak], axis=-1), mlstm_conv_w[l], mlstm_conv_b[l]))
        mq, mk = jnp.split(qk, 2, axis=-1)
        hcell = mlstm_chunkwise(mq.reshape(b, s, H_A, DH_A) * DH_A ** -0.5, mk.reshape(b, s, H_A, DH_A),
                                av.reshape(b, s, H_A, DH_A), ai, jax.nn.log_sigmoid(af))
        hf = hcell.astype(jnp.float32)
        mu = jnp.mean(hf, axis=-1, keepdims=True)
        hf = (hf - mu) * lax.rsqrt(jnp.mean(jnp.square(hf - mu), axis=-1, keepdims=True) + 1e-5)
        hn = hf.reshape(b, s, W_A).astype(x.dtype) * mlstm_norm[l]
        y_a = jax.nn.sigmoid(ao) * hn * jax.nn.silu(az)

        u = causal_dwconv(ba * jax.nn.sigmoid(bb), conf_dw_w[l], conf_dw_b[l])
        y_b = jax.nn.silu(layernorm(u, conf_ln_g[l], conf_ln_b[l])) * jax.nn.silu(bz)

        hc = nsa(cq.reshape(b, s, H_C, DH_C),
                 ckc.reshape(b, s, G_C, DH_C), cvc.reshape(b, s, G_C, DH_C),
                 cks.reshape(b, s, G_C, DH_C), cvs.reshape(b, s, G_C, DH_C),
                 ckw.reshape(b, s, G_C, DH_C), cvw.reshape(b, s, G_C, DH_C),
                 cg.reshape(b, s, H_C, 3), nsa_cmp_pe[l], nsa_cmp_w1[l], nsa_cmp_w2[l])
        y_c = hc * jax.nn.silu(cz)

        g = jax.nn.sigmoid(mg).reshape(b, s, N_BRANCH, D_MODEL)
        merged = (g[:, :, 0] * (y_a @ w_branch[l, 0]) + g[:, :, 1] * (y_b @ w_branch[l, 1])
                  + g[:, :, 2] * (y_c @ w_branch[l, 2]))
        out = merged @ w_out[l]
        x = x + gate[:, None] * rmsnorm(out, norm_post[l])
    return x
```

```python
import numpy as np
import ml_dtypes
import concourse.bass as bass
import concourse.mybir as mybir
from concourse.bass_utils import run_bass_kernel_spmd

F32 = mybir.dt.float32
BF16 = mybir.dt.bfloat16
AF = mybir.ActivationFunctionType
ALU = mybir.AluOpType
AX = mybir.AxisListType

ENGS = ("pe", "act", "dve", "pool", "sp")

D = 1024
NEGM = -30000.0
EPOCH_N = 30000
DHS = 128 ** -0.5


class Tile:
    __slots__ = ("t", "name", "lw", "dlw", "rd", "dsems", "slot", "nowaw", "excl")

    def __init__(self, t, name):
        self.t = t
        self.name = name
        self.lw = None
        self.dlw = {}
        self.rd = {}
        self.dsems = []
        self.slot = 0
        self.nowaw = False
        self.excl = False

    def __getitem__(self, k):
        return self.t[k]


class FMSet(Tile):
    __slots__ = ("groups",)

    def __init__(self, groups, name):
        Tile.__init__(self, None, name)
        self.groups = groups
        self.nowaw = True

    def __getitem__(self, k):
        if not isinstance(k, tuple):
            k = (k,)
        c = k[0]
        rest = k[1:]
        if isinstance(c, slice):
            c0, c1 = c.start, c.stop
            g = c0 // 8
            assert (c1 - 1) // 8 == g, (c0, c1)
            return self.groups[g][(slice(c0 - 8 * g, c1 - 8 * g),) + rest]
        g = c // 8
        return self.groups[g][(c - 8 * g,) + rest]


class Prog:
    def __init__(self, nc):
        self.nc = nc
        self.ops = {e: [] for e in ENGS}
        self.cnt = {e: 0 for e in ENGS}
        self.waited = {e: {} for e in ENGS}
        self.sems = {}
        self.ctx = []
        self.ndsem = 0
        self.dsem_cnt = {}
        self.free_dsems = []
        self.uid = 0
        self.epoch = {e: 0 for e in ENGS}
        self.ecnt = {e: 0 for e in ENGS}

    def _enter(self, cm):
        v = cm.__enter__()
        self.ctx.append(cm)
        return v

    def sem(self, key):
        if key not in self.sems:
            self.sems[key] = self._enter(self.nc.semaphore("s_%s" % (str(key).replace(" ", ""),)))
        return self.sems[key]

    def sbuf(self, name, shape, dt):
        self.uid += 1
        name = "%s_u%d" % (name, self.uid)
        return Tile(self._enter(self.nc.sbuf_tensor(name, list(shape), dt)), name)

    def psum(self, name, shape, dt=F32):
        self.uid += 1
        name = "%s_u%d" % (name, self.uid)
        t = Tile(self._enter(self.nc.psum_tensor(name, list(shape), dt)), name)
        t.excl = True
        return t

    def dram(self, name, shape, dt, kind="Internal"):
        t = Tile(self.nc.dram_tensor(name, list(shape), dt, kind=kind).ap(), name)
        t.nowaw = True
        return t

    def view(self, ap, name):
        return Tile(ap, name)

    def mark(self):
        return len(self.ctx)

    def release(self, mark, tiles):
        self.barrier()
        for t in tiles:
            self.free_dsems.extend(t.dsems)
            t.dsems = []
        while len(self.ctx) > mark:
            self.ctx.pop().__exit__(None, None, None)

    def barrier(self):
        tot = {("e", (e, self.epoch[e])): self.ecnt[e] for e in ENGS if self.ecnt[e] > 0}
        for k, v in self.dsem_cnt.items():
            if v > 0:
                tot[("d", k)] = v
        for e in ENGS:
            waits = {}
            for k, v in tot.items():
                if k[0] == "e" and k[1][0] == e and e in ("pe", "sp"):
                    continue
                if self.waited[e].get(k, 0) < v:
                    waits[k] = v
                    self.waited[e][k] = v
            if waits:
                self.ops[e].append((waits, None, None, 0))

    def _need(self, eng, dep, waits):
        if dep is None:
            return
        kind, key, val = dep
        if kind == "e" and key[0] == eng and eng in ("pe", "sp"):
            return
        k = (kind, key)
        if self.waited[eng].get(k, 0) >= val:
            return
        if waits.get(k, 0) < val:
            waits[k] = val

    def _need_writes(self, eng, t, waits):
        self._need(eng, t.lw, waits)
        for k, v in t.dlw.items():
            self._need(eng, ("d", k, v), waits)

    def _commit(self, eng, waits):
        for k, v in waits.items():
            self.waited[eng][k] = v

    def op(self, eng, fn, reads=(), writes=()):
        waits = {}
        for t in reads:
            self._need_writes(eng, t, waits)
            if t.excl:
                for k, v in t.rd.items():
                    if not (k[0] == "e" and k[1][0] == eng):
                        self._need(eng, (k[0], k[1], v), waits)
        for t in writes:
            self._need_writes(eng, t, waits)
            for k, v in t.rd.items():
                self._need(eng, (k[0], k[1], v), waits)
        self._commit(eng, waits)
        self.cnt[eng] += 1
        if self.ecnt[eng] >= EPOCH_N:
            self.epoch[eng] += 1
            self.ecnt[eng] = 0
        self.ecnt[eng] += 1
        ek = (eng, self.epoch[eng])
        idx = self.ecnt[eng]
        self.ops[eng].append((waits, fn, ("e", ek), 1))
        for t in writes:
            t.lw = ("e", ek, idx)
            t.dlw = {}
            t.rd = {}
        for t in reads:
            if t.lw is not None and t.lw == ("e", ek, idx):
                continue
            t.rd[("e", ek)] = idx
        return idx

    def _new_dsem(self, kind):
        fl = [d for d in self.free_dsems if d.startswith(kind) and self.dsem_cnt[d] < 40000]
        if fl:
            self.free_dsems.remove(fl[-1])
            return fl[-1]
        d = "%sdma%d" % (kind, self.ndsem)
        self.ndsem += 1
        self.sem(d)
        self.dsem_cnt[d] = 0
        return d

    def dma(self, q, fn, src, dst):
        kind = "sw" if q == "pool" else "hw"
        nslots = 4 if dst.nowaw else 1
        while len(dst.dsems) < nslots:
            dst.dsems.append(self._new_dsem(kind))
        for d in dst.dsems:
            assert d.startswith(kind), (dst.name, d, q)
        waits = {}
        self._need_writes(q, src, waits)
        if dst.nowaw:
            ds = dst.dsems[dst.slot % nslots]
            dst.slot += 1
            self._need(q, dst.lw, waits)
        else:
            ds = dst.dsems[0]
            self._need_writes(q, dst, waits)
        if self.dsem_cnt[ds] > 0:
            self._need(q, ("d", ds, self.dsem_cnt[ds]), waits)
        for k, v in dst.rd.items():
            self._need(q, (k[0], k[1], v), waits)
        self._commit(q, waits)
        self.dsem_cnt[ds] += 16
        v = self.dsem_cnt[ds]
        self.ops[q].append((waits, fn, ("d", ds), 16))
        if dst.nowaw:
            dst.dlw[ds] = v
        else:
            dst.dlw = {ds: v}
        dst.lw = None
        dst.rd = {}
        src.rd[("d", ds)] = v

    def wait_all(self, eng, tiles):
        waits = {}
        for t in tiles:
            self._need_writes(eng, t, waits)
        self._commit(eng, waits)
        self.ops[eng].append((waits, None, None, 0))

    def emit(self):
        nc = self.nc
        for e in ENGS:
            for ep in range(self.epoch[e] + 1):
                self.sem(("e", (e, ep)))
        semh = self.sems
        print("n semaphores", len(semh), flush=True)

        def key2sem(k):
            if k[0] == "e":
                return semh[("e", k[1])]
            return semh[k[1]]

        def run(engobj, name):
            for waits, fn, inc, amt in self.ops[name]:
                for k, v in waits.items():
                    engobj.wait_ge(key2sem(k), v)
                if fn is None:
                    continue
                fn(engobj).then_inc(key2sem(inc), amt)

        with nc.Block() as block:
            @block.tensor
            def _(e):
                run(e, "pe")

            @block.scalar
            def _(e):
                run(e, "act")

            @block.vector
            def _(e):
                run(e, "dve")

            @block.gpsimd
            def _(e):
                run(e, "pool")

            @block.sync
            def _(e):
                run(e, "sp")

    def close(self):
        while self.ctx:
            self.ctx.pop().__exit__(None, None, None)


_SPL = [1024, 1024, 1024, 1024, 1024, 4, 4, 1024, 1024, 1024, 1024, 256, 256, 256, 256, 256, 256, 24, 1024, 3072]
_NAMES = ["aq", "ak", "av", "ao", "az", "ai", "af", "ba", "bb", "bz", "cq", "ckc", "cvc", "cks", "cvs", "ckw", "cvw",
          "cg", "cz", "mg"]
_OFF = dict(zip(_NAMES, np.cumsum([0] + _SPL[:-1]).tolist()))
_W = dict(zip(_NAMES, _SPL))
FM_ORDER = ["aq", "ak", "ba", "bb", "bz", "ao", "az", "cq", "ckc", "cvc", "cks", "ckw", "cz", "mg"]
TM_ORDER = ["av", "cvs", "cvw"]
FMC = {}
_c = 0
for _n in FM_ORDER:
    FMC[_n] = _c
    _c += _W[_n] // 128
NFM = _c
NFMC = NFM * 128
NTM = 1536
PERM = np.concatenate([np.arange(_OFF[n], _OFF[n] + _W[n]) for n in FM_ORDER + TM_ORDER + ["ai", "af", "cg"]])
N_IN = 14880
assert PERM.shape[0] == N_IN


def make_consts(S):
    bf = ml_dtypes.bfloat16
    p = np.arange(128)
    c = {}
    c["identb"] = np.eye(128, dtype=np.float32).astype(bf)
    c["ident32"] = np.eye(128, dtype=np.float32)
    c["ones32"] = np.ones((128, 128), np.float32)
    c["onesb"] = np.ones((128, 128), np.float32).astype(bf)
    oh = np.zeros((4, 4, 128), np.float32)
    for h in range(4):
        oh[h, h, :] = 1.0
    c["onehot4"] = oh
    c["cmaskT"] = (p[:, None] <= p[None, :]).astype(np.float32)
    caus = np.where(p[:, None] <= p[None, :], 0.0, NEGM).astype(np.float32)
    band = np.where(p[:, None] > p[None, :], 0.0, NEGM).astype(np.float32)
    c["causneg"] = np.repeat(caus[:, None, :], 4, axis=1).astype(bf)
    c["bandneg"] = np.repeat(band[:, None, :], 4, axis=1).astype(bf)
    cm = np.zeros((128, 17, 4, 128), np.float32)
    for idx in range(17):
        vis = (16 * (p[:, None] - 8 * idx) + 31) <= p[None, :]
        cm[:, idx, :, :] = np.where(vis, 0.0, NEGM)[:, None, :]
    c["cmpneg"] = cm.astype(bf)
    n = np.arange(512)
    m = np.arange(128)
    ov = ((16 * n[:, None] < 64 * m[None, :] + 64) & (16 * n[:, None] + 32 > 64 * m[None, :])).astype(np.float32)
    ov[:, 0] = 1.0
    c["ovl"] = np.ascontiguousarray(ov.reshape(4, 128, 128).transpose(1, 0, 2)).astype(bf)
    xs = np.arange(S)
    c["gmat"] = (xs[None, :] // 64 == m[:, None]).astype(np.float32).astype(bf)
    X = np.arange(255) - 127
    r = np.arange(128)
    rel = X[None, :] - (r[:, None] >= 64)
    bm = np.where(rel > 0, -1e4, np.where((rel == 0) | (rel == -1), 1e4, 0.0)).astype(np.float32)
    c["bm"] = bm
    return c


CONST_SPECS = [("identb", (128, 128), BF16), ("ident32", (128, 128), F32), ("ones32", (128, 128), F32),
               ("onesb", (128, 128), BF16), ("onehot4", (4, 4, 128), F32), ("cmaskT", (128, 128), F32),
               ("causneg", (128, 4, 128), BF16), ("bandneg", (128, 4, 128), BF16),
               ("cmpneg", (128, 17, 4, 128), BF16), ("ovl", (128, 4, 128), BF16), ("gmat", None, BF16),
               ("bm", (128, 255), F32)]


def build(S, L, dbg=False, phases="01ABCT"):
    nc = bass.Bass("TRN2", target_bir_lowering=False)
    P = Prog(nc)
    NT = S // 128
    TB = min(S, 2048)
    NSB = TB // 512
    NTBK = S // TB

    def din(name, shape, dt=F32):
        return P.view(nc.dram_tensor(name, list(shape), dt, kind="ExternalInput").ap(), name)

    xT_in = din("xT", [8, 128, S])
    cT = din("cT", [128, 2, 8])
    w_ada = din("w_ada", [L, 1024, 3072])
    b_ada = din("b_ada", [L, 128, 24])
    g_pre = din("g_pre", [L, 128, 8])
    g_post = din("g_post", [L, 128, 8])
    w_in = din("w_in", [L, 1024, N_IN])
    b_fm = din("b_fm", [L, 128, NFM])
    b_tm = din("b_tm", [L, 1, NTM + 24])
    b_sm = din("b_sm", [L, 8, 1])
    cvw_a = din("cvw_a", [L, 128, 16, 4])
    cvb_a = din("cvb_a", [L, 128, 16])
    gn_a = din("gn_a", [L, 128, 8])
    dww = din("dww", [L, 128, 8, 31])
    dwb = din("dwb", [L, 128, 8])
    lng = din("lng", [L, 128, 8])
    lnb = din("lnb", [L, 128, 8])
    pe_c = din("pe_c", [L, 2, 128, 32])
    w1_c = din("w1_c", [L, 2, 4096, 256])
    w2_c = din("w2_c", [L, 2, 256, 128])
    w_br = din("w_br", [L, 3, 1024, 1024])
    w_o = din("w_o", [L, 1024, 1024])
    cin = {}
    for nm, shp, dt in CONST_SPECS:
        if nm == "gmat":
            shp = (128, S)
        cin[nm] = din("c_" + nm, shp, dt)
    outT = P.view(nc.dram_tensor("outT", [8, 128, S], F32, kind="ExternalOutput").ap(), "outT")
    outT.nowaw = True

    fm = FMSet([nc.dram_tensor("fm%d" % gi, [8, 128, S], BF16, kind="Internal").ap() for gi in range(NFM // 8)], "fm")
    tm = P.dram("tm", [S, NTM], BF16)
    cgs = P.dram("cgs", [S, 24], F32)
    gif = P.dram("gif", [8, S], F32)
    yT = fm
    xs_list = [xT_in] + [outT] * L

    def mm(ot, o, lt, l_, rt, r, start=True, stop=True):
        P.op("pe", lambda e: e.matmul(o, lhsT=l_, rhs=r, start=start, stop=stop), [lt, rt], [ot])

    def tr(ot, o, it, i_, idt, idn):
        P.op("pe", lambda e: e.transpose(out=o, in_=i_, identity=idn), [it, idt], [ot])

    def act(ot, o, it, i_, func, bias=None, scale=None, extra=()):
        kw = {}
        if bias is not None:
            kw["bias"] = bias
        if scale is not None:
            kw["scale"] = scale
        P.op("act", lambda e: e.activation(out=o, in_=i_, func=func, **kw), [it] + list(extra), [ot])

    def tt(eng, ot, o, at, a, bt, b, op):
        P.op(eng, lambda e: e.tensor_tensor(out=o, in0=a, in1=b, op=op), [at, bt], [ot])

    def ts(eng, ot, o, at, a, s1, op0, s2=None, op1=None, extra=()):
        if op1 is None:
            P.op(eng, lambda e: e.tensor_scalar(out=o, in0=a, scalar1=s1, scalar2=None, op0=op0), [at] + list(extra), [ot])
        else:
            P.op(eng, lambda e: e.tensor_scalar(out=o, in0=a, scalar1=s1, scalar2=s2, op0=op0, op1=op1),
                 [at] + list(extra), [ot])

    def stt(ot, o, at, a, sc, bt, b, op0, op1, extra=()):
        P.op("dve", lambda e: e.scalar_tensor_tensor(out=o, in0=a, scalar=sc, in1=b, op0=op0, op1=op1),
             [at, bt] + list(extra), [ot])

    def cp(eng, ot, o, it, i_):
        if eng == "act":
            P.op("act", lambda e: e.copy(out=o, in_=i_), [it], [ot])
        else:
            P.op(eng, lambda e: e.tensor_copy(out=o, in_=i_), [it], [ot])

    def memset(eng, t, ap, v):
        P.op(eng, lambda e: e.memset(ap, v), [], [t])

    def ld(q, dst, d, src, s):
        P.dma(q, lambda e: e.dma_start(out=d, in_=s), src, dst)

    def recip(ot, o, it, i_):
        P.op("dve", lambda e: e.reciprocal(out=o, in_=i_), [it], [ot])

    C = {}
    for nm, shp, dt in CONST_SPECS:
        if nm == "gmat":
            shp = (128, S)
        C[nm] = P.sbuf("k_" + nm, shp, dt)
        if nm == "gmat":
            for c0 in range(0, S, 2048):
                ld("sp", C[nm], C[nm][:, c0:c0 + 2048], cin[nm], cin[nm][:, c0:c0 + 2048])
        else:
            ld("sp", C[nm], C[nm][:], cin[nm], cin[nm][:])
    identb, ident32, ones32, onesb = C["identb"], C["ident32"], C["ones32"], C["onesb"]

    cTs = P.sbuf("cTs", [128, 2, 8], F32)
    ld("sp", cTs, cTs[:], cT, cT[:])
    eps6 = P.sbuf("eps6", [128, 1], F32)
    memset("pool", eps6, eps6[:], 1e-6)
    eps5 = P.sbuf("eps5", [128, 1], F32)
    memset("pool", eps5, eps5[:], 1e-5)
    one1 = P.sbuf("one1", [128, 1], F32)
    memset("pool", one1, one1[:], 1.0)

    dbg_out = {}

    for l in range(L):
        x_src = xs_list[l]
        x_dst = xs_list[l + 1]
        mk_layer = P.mark()
        lay_tiles = []

        def lsb(name, shape, dt):
            t = P.sbuf("%s_%d" % (name, l), shape, dt)
            lay_tiles.append(t)
            return t

        Aada = lsb("Aada", [128, 8], F32)
        Bada = lsb("Bada", [128, 8], F32)
        Gada = lsb("Gada", [128, 8], F32)
        mk0 = P.mark()
        t0 = []
        wad = [P.sbuf("wad%d" % i, [128, 8, 512], F32) for i in range(2)]
        ada = P.sbuf("ada", [128, 24], F32)
        bad = P.sbuf("bad", [128, 24], F32)
        gp = P.sbuf("gp", [128, 8], F32)
        gq = P.sbuf("gq", [128, 8], F32)
        psa = P.psum("psa", [128, 256, 2], F32)
        t0 += wad + [ada, bad, gp, gq]
        ld("sp", bad, bad[:], b_ada, b_ada[l])
        ld("sp", gp, gp[:], g_pre, g_pre[l])
        ld("sp", gq, gq[:], g_post, g_post[l])
        wav = w_ada[l].rearrange("(k p) n -> p k n", p=128)
        for cb in range(6):
            wt = wad[cb % 2]
            ld("sp", wt, wt[:], w_ada, wav[:, :, cb * 512:(cb + 1) * 512])
            for cc in range(4):
                oc = cb * 4 + cc
                for k in range(8):
                    mm(psa, psa[:, oc, :], wt, wt[:, k, cc * 128:(cc + 1) * 128], cTs, cTs[:, :, k],
                       start=(k == 0), stop=(k == 7))
        tt("dve", ada, ada[:], psa, psa[:, 0:24, 0], bad, bad[:], ALU.add)
        cp("dve", Bada, Bada[:], ada, ada[:, 0:8])
        ts("dve", Aada, Aada[:], ada, ada[:, 8:16], 1.0, ALU.add)
        tt("dve", Aada, Aada[:], Aada, Aada[:], gp, gp[:], ALU.mult)
        tt("dve", Gada, Gada[:], ada, ada[:, 16:24], gq, gq[:], ALU.mult)
        P.release(mk0, t0)

        mk1 = P.mark()
        t1 = []

        def sb1(name, shape, dt):
            t = P.sbuf(name, shape, dt)
            t1.append(t)
            return t

        bfm = sb1("bfm", [128, NFM], F32)
        ld("sp", bfm, bfm[:], b_fm, b_fm[l])
        btm = sb1("btm", [128, NTM + 24], F32)
        ld("sp", btm, btm[:], b_tm, b_tm[l].partition_broadcast(128).rearrange("p o n -> p (o n)"))
        bsm = sb1("bsm", [8, 1], F32)
        ld("sp", bsm, bsm[:], b_sm, b_sm[l])
        hT = sb1("hT", [128, 8, TB], BF16)
        xblk = sb1("xblk", [128, 8, 512], F32)
        sqb = sb1("sqb", [128, 8, 512], F32)
        rstd = sb1("rstd", [128, 512], F32)
        wbuf = [sb1("wbuf%d" % i, [128, 8, 512], BF16) for i in range(2)]
        wsm = sb1("wsm", [128, 8, 32], BF16)
        stg = [sb1("stg%d" % i, [128, 4, 512], BF16) for i in range(2)]
        stgs = [sb1("stgs%d" % i, [8, 512], F32) for i in range(2)]
        stgc = [sb1("stgc%d" % i, [128, 4, 24], F32) for i in range(2)]
        pss = P.psum("pss", [128, 512], F32)
        psp = [P.psum("psp%d" % i, [128, 512], F32) for i in range(4)]
        pssm = P.psum("pssm", [128, 512], F32)
        wv = w_in[l].rearrange("(k p) n -> p k n", p=128)
        ld("pool", wsm, wsm[:], w_in, wv[:, :, NFMC + NTM:NFMC + NTM + 32])
        nstg = 0
        npsp = 0
        for tb in range(NTBK):
            for sb_ in range(NSB):
                t0_ = tb * TB + sb_ * 512
                ld("sp", xblk, xblk[:], x_src, x_src[:, :, t0_:t0_ + 512].rearrange("k p s -> p k s"))
                act(sqb, sqb[:], xblk, xblk[:], AF.Square)
                for k in range(8):
                    mm(pss, pss[:], ones32, ones32[:], sqb, sqb[:, k, :], start=(k == 0), stop=(k == 7))
                act(rstd, rstd[:], pss, pss[:], AF.Sqrt, bias=eps6[:], scale=1.0 / D, extra=[eps6])
                recip(rstd, rstd[:], rstd, rstd[:])
                tt("dve", sqb, sqb[:], xblk, xblk[:], rstd, rstd[:].unsqueeze(1).broadcast_to([128, 8, 512]), ALU.mult)
                for k in range(8):
                    act(hT, hT[:, k, sb_ * 512:(sb_ + 1) * 512], sqb, sqb[:, k, :], AF.Identity,
                        bias=Bada[:, k:k + 1], scale=Aada[:, k:k + 1], extra=[Aada, Bada])
            for cb in range(NFM // 4):
                wt = wbuf[cb % 2]
                ld("pool", wt, wt[:], w_in, wv[:, :, cb * 512:(cb + 1) * 512])
                for sb_ in range(NSB):
                    t0_ = tb * TB + sb_ * 512
                    st = stg[nstg % 2]
                    nstg += 1
                    for cc in range(4):
                        ps = psp[npsp % 4]
                        npsp += 1
                        ch = cb * 4 + cc
                        for k in range(8):
                            mm(ps, ps[:], wt, wt[:, k, cc * 128:(cc + 1) * 128], hT, hT[:, k, sb_ * 512:(sb_ + 1) * 512],
                               start=(k == 0), stop=(k == 7))
                        act(st, st[:, cc, :], ps, ps[:], AF.Identity, bias=bfm[:, ch:ch + 1], extra=[bfm])
                    ld("sp", fm, fm[cb * 4:cb * 4 + 4, :, t0_:t0_ + 512].rearrange("c p s -> p c s"), st, st[:])
            for cb in range(3):
                wt = wbuf[(NFM // 4 + cb) % 2]
                ld("pool", wt, wt[:], w_in, wv[:, :, NFMC + cb * 512:NFMC + (cb + 1) * 512])
                for sb_ in range(NSB):
                    t0_ = tb * TB + sb_ * 512
                    st = stg[nstg % 2]
                    nstg += 1
                    for tl in range(4):
                        ps = psp[npsp % 4]
                        npsp += 1
                        c0 = sb_ * 512 + tl * 128
                        for k in range(8):
                            mm(ps, ps[:], hT, hT[:, k, c0:c0 + 128], wt, wt[:, k, :], start=(k == 0), stop=(k == 7))
                        tt("dve", st, st[:, tl, :], ps, ps[:], btm, btm[:, cb * 512:(cb + 1) * 512], ALU.add)
                    ld("sp", tm, tm[t0_:t0_ + 512, cb * 512:(cb + 1) * 512].rearrange("(t p) c -> p t c", p=128), st, st[:])
            for sb_ in range(NSB):
                t0_ = tb * TB + sb_ * 512
                ss_ = stgs[sb_ % 2]
                for k in range(8):
                    mm(pssm, pssm[0:8, :], wsm, wsm[:, k, 0:8], hT, hT[:, k, sb_ * 512:(sb_ + 1) * 512],
                       start=(k == 0), stop=(k == 7))
                act(ss_, ss_[:], pssm, pssm[0:8, :], AF.Identity, bias=bsm[:], extra=[bsm])
                ld("sp", gif, gif[:, t0_:t0_ + 512], ss_, ss_[:])
                sc_ = stgc[sb_ % 2]
                for tl in range(4):
                    c0 = sb_ * 512 + tl * 128
                    for k in range(8):
                        mm(pssm, pssm[:, 0:24], hT, hT[:, k, c0:c0 + 128], wsm, wsm[:, k, 8:32], start=(k == 0),
                           stop=(k == 7))
                    tt("dve", sc_, sc_[:, tl, :], pssm, pssm[:, 0:24], btm, btm[:, NTM:NTM + 24], ALU.add)
                ld("sp", cgs, cgs[t0_:t0_ + 512, :].rearrange("(t p) c -> p t c", p=128), sc_, sc_[:])
        P.release(mk1, t1)

        if dbg and l == 0:
            dbg_out["tm"] = tm
            dbg_out["gif"] = gif
            dbg_out["cgs"] = cgs

        if "A" in phases:
            phase_mlstm(P, nc, S, l, C, fm, tm, gif, yT, cvw_a, cvb_a, gn_a, eps5, one1,
                        mm, tr, act, tt, ts, stt, cp, memset, ld, recip)
        if "B" in phases:
            phase_conv(P, nc, S, l, C, fm, yT, dww, dwb, lng, lnb, eps5,
                       mm, tr, act, tt, ts, stt, cp, memset, ld, recip)
        if "C" in phases:
            phase_nsa(P, nc, S, l, C, fm, tm, cgs, yT, pe_c, w1_c, w2_c,
                      mm, tr, act, tt, ts, stt, cp, memset, ld, recip)
        if "T" in phases:
            phase_tail(P, nc, S, l, C, fm, yT, w_br, w_o, x_src, x_dst, Gada, eps6,
                       mm, tr, act, tt, ts, stt, cp, memset, ld, recip)
        P.release(mk_layer, lay_tiles)

    dbg_tiles = []
    if dbg:
        for nm, t in dbg_out.items():
            shp = list(t.t.shape)
            o = P.view(nc.dram_tensor("dbg_" + nm, shp, t.t.dtype, kind="ExternalOutput").ap(), "dbg_" + nm)
            ld("sp", o, o[:], t, t[:])
            dbg_tiles.append(o)
    P.wait_all("sp", [outT] + dbg_tiles)
    print("op counts", {e: len(P.ops[e]) for e in ENGS}, "waits", {e: sum(len(o[0]) for o in P.ops[e]) for e in ENGS}, flush=True)
    P.emit()
    P.close()
    return nc


def phase_mlstm(P, nc, S, l, C, fm, tm, gif, yT, cvw_a, cvb_a, gn_a, eps5, one1,
                mm, tr, act, tt, ts, stt, cp, memset, ld, recip):
    NT = S // 128
    identb, ident32 = C["identb"], C["ident32"]
    mk = P.mark()
    tl_ = []

    def sb(name, shape, dt):
        t = P.sbuf(name, shape, dt)
        tl_.append(t)
        return t

    SEG = min(S, 2048)
    cw = sb("cw", [128, 16, 4], F32)
    cb_ = sb("cb", [128, 16], F32)
    ld("sp", cw, cw[:], cvw_a, cvw_a[l])
    ld("sp", cb_, cb_[:], cvb_a, cvb_a[l])
    buf = [sb("cbuf%d" % i, [128, 3 + SEG], BF16) for i in range(2)]
    acc = sb("cacc", [128, SEG], F32)
    ob = [sb("cob%d" % i, [128, SEG], BF16) for i in range(2)]
    it = 0
    for ch in range(16):
        for sg in reversed(range(S // SEG)):
            b = buf[it % 2]
            o = ob[it % 2]
            it += 1
            t0 = sg * SEG
            if sg == 0:
                memset("pool", b, b[:, 0:3], 0.0)
                ld("sp", b, b[:, 3:3 + SEG], fm, fm[ch, :, 0:SEG])
            else:
                ld("sp", b, b[:], fm, fm[ch, :, t0 - 3:t0 + SEG])
            ts("dve", acc, acc[:], b, b[:, 0:SEG], cw[:, ch, 0:1], ALU.mult, cb_[:, ch:ch + 1], ALU.add, extra=[cw, cb_])
            for j in range(1, 4):
                stt(acc, acc[:], b, b[:, j:j + SEG], cw[:, ch, j:j + 1], acc, acc[:], ALU.mult, ALU.add, extra=[cw])
            act(o, o[:], acc, acc[:], AF.Silu)
            if ch >= 8:
                ts("pool", o, o[:], o, o[:], 1.0 / 16.0, ALU.mult)
            ld("sp", fm, fm[ch, :, t0:t0 + SEG], o, o[:])
    P.release(mk, tl_)

    mk = P.mark()
    tl_ = []
    eT = sb("eT", [128, NT, 4], F32)
    thrT = sb("thrT", [128, NT, 4], F32)
    decb = sb("decb", [128, 4, NT], F32)
    mk2 = P.mark()
    tl2 = []

    def sb2(name, shape, dt):
        t = P.sbuf(name, shape, dt)
        tl2.append(t)
        return t

    ai = sb2("g_ai", [4, S], F32)
    af = sb2("g_af", [4, S], F32)
    nbg = sb2("g_nbg", [4, S], F32)
    U = sb2("g_U", [4, S], F32)
    uprev = sb2("g_up", [4, NT], F32)
    dec = sb2("g_dec", [4, NT], F32)
    pst = P.psum("g_pst", [128, 512], F32)
    psd = P.psum("g_psd", [128, 4, 128], F32)
    for c0 in range(0, S, 1024):
        ld("sp", ai, ai[:, c0:c0 + 1024], gif, gif[0:4, c0:c0 + 1024])
        ld("sp", af, af[:, c0:c0 + 1024], gif, gif[4:8, c0:c0 + 1024])
    act(af, af[:], af, af[:], AF.Exp, scale=-1.0)
    act(af, af[:], af, af[:], AF.Ln, bias=one1[0:4, :], extra=[one1])
    P.op("dve", lambda e: e.tensor_tensor_scan(out=nbg[:], data0=one1[0:4, 0:1].broadcast_to([4, S]), data1=af[:],
                                               initial=0.0, op0=ALU.mult, op1=ALU.add), [af, one1], [nbg])
    tt("dve", ai, ai[:], ai, ai[:], nbg, nbg[:], ALU.add)
    P.op("dve", lambda e: e.tensor_tensor_scan(out=U[:], data0=ai[:], data1=ai[:], initial=0.0, op0=ALU.max,
                                               op1=ALU.max), [ai], [U])
    memset("dve", uprev, uprev[:, 0:1], 0.0)
    U3 = U[:].rearrange("h (c t) -> h c t", t=128)
    if NT > 1:
        cp("dve", uprev, uprev[:, 1:NT], U, U3[:, 0:NT - 1, 127])
    tt("dve", dec, dec[:], uprev, uprev[:], U, U3[:, :, 127], ALU.subtract)
    act(dec, dec[:], dec, dec[:], AF.Exp)
    a3 = ai[:].rearrange("h (c t) -> h c t", t=128)
    n3 = nbg[:].rearrange("h (c t) -> h c t", t=128)
    upb = uprev[:].unsqueeze(2).broadcast_to([4, NT, 128])
    tt("dve", ai, a3, ai, a3, uprev, upb, ALU.subtract)
    act(ai, ai[:], ai, ai[:], AF.Exp)
    tt("dve", nbg, n3, nbg, n3, uprev, upb, ALU.subtract)
    act(nbg, nbg[:], nbg, nbg[:], AF.Exp)
    for i in range(NT):
        tr(pst, pst[:, i * 4:(i + 1) * 4], ai, ai[:, i * 128:(i + 1) * 128], ident32, ident32[0:4, 0:4])
    cp("dve", eT, eT[:].rearrange("p c h -> p (c h)"), pst, pst[:, 0:NT * 4])
    for i in range(NT):
        tr(pst, pst[:, i * 4:(i + 1) * 4], nbg, nbg[:, i * 128:(i + 1) * 128], ident32, ident32[0:4, 0:4])
    cp("dve", thrT, thrT[:].rearrange("p c h -> p (c h)"), pst, pst[:, 0:NT * 4])
    oh = C["onehot4"]
    for h in range(4):
        mm(psd, psd[:, h, 0:NT], oh, oh[:, h, :], dec, dec[:])
    cp("dve", decb, decb[:], psd, psd[:, :, 0:NT])
    P.release(mk2, tl2)

    gain = sb("gain", [128, 8], F32)
    ld("sp", gain, gain[:], gn_a, gn_a[l])
    cmask = C["cmaskT"]
    q4 = [sb("q4_%d" % i, [128, 8, 512], BF16) for i in range(2)]
    k4 = [sb("k4_%d" % i, [128, 8, 512], BF16) for i in range(2)]
    v4 = [sb("v4_%d" % i, [128, 4, 1024], BF16) for i in range(2)]
    ao4 = [sb("ao4_%d" % i, [128, 8, 512], BF16) for i in range(1)] * 2
    az4 = [sb("az4_%d" % i, [128, 8, 512], BF16) for i in range(1)] * 2
    s1 = sb("s1", [128, 8, 512], BF16)
    og = sb("og", [128, 8, 512], BF16)
    vaug = [sb("vaug%d" % i, [128, 4, 257], BF16) for i in range(2)]
    ev = [sb("ev%d" % i, [128, 4, 257], BF16) for i in range(2)]
    ktm = [sb("ktm%d" % i, [128, 8, 128], BF16) for i in range(2)]
    Dst = [[sb("D_%d_%d" % (h, c), [128, 257], F32) for c in range(2)] for h in range(4)]
    Cb = [[sb("Cb_%d_%d" % (h, c), [128, 257], BF16) for c in range(2)] for h in range(4)]
    sm = [sb("sm%d" % i, [128, 128], BF16) for i in range(2)]
    den = [sb("den%d" % i, [128, 1], F32) for i in range(2)]
    hs = [sb("hs%d" % i, [128, 256], F32) for i in range(2)]
    bst = [sb("bst%d" % i, [128, 6], F32) for i in range(2)]
    mv = [sb("mv%d" % i, [128, 2], F32) for i in range(2)]
    rs = [sb("rs%d" % i, [128, 1], F32) for i in range(2)]
    hn = [sb("hn%d" % i, [128, 8, 128], BF16) for i in range(2)]
    yst = [sb("yst%d" % i, [128, 8, 512], BF16) for i in range(2)]
    psS_ = P.psum("psS", [128, 512], F32)
    psX_ = P.psum("psX", [128, 512], F32)
    psC_ = [P.psum("psC%d" % i, [128, 512], F32) for i in range(2)]
    psK = P.psum("psK", [128, 8, 128], BF16)
    psT = P.psum("psT", [128, 8, 128], BF16)
    for i in range(2):
        memset("pool", vaug[i], vaug[i][:, :, 256:257], 1.0)
    n = 0
    for i in range(NT):
        g4 = i // 4
        tl = i % 4
        b4 = g4 % 2
        cs = slice(tl * 128, (tl + 1) * 128)
        if tl == 0:
            t0 = g4 * 512
            ld("sp", q4[b4], q4[b4][:], fm, fm[0:8, :, t0:t0 + 512].rearrange("c p s -> p c s"))
            ld("sp", k4[b4], k4[b4][:], fm, fm[8:16, :, t0:t0 + 512].rearrange("c p s -> p c s"))
            ld("sp", v4[b4], v4[b4][:], tm, tm[t0:t0 + 512, 0:1024].rearrange("(t p) c -> p t c", p=128))
            a0 = FMC["ao"]
            z0 = FMC["az"]
            ld("sp", ao4[b4], ao4[b4][:], fm, fm[a0:a0 + 8, :, t0:t0 + 512].rearrange("c p s -> p c s"))
            ld("sp", az4[b4], az4[b4][:], fm, fm[z0:z0 + 8, :, t0:t0 + 512].rearrange("c p s -> p c s"))
            act(s1, s1[:], ao4[b4], ao4[b4][:], AF.Sigmoid)
            act(og, og[:], az4[b4], az4[b4][:], AF.Silu)
            tt("pool", og, og[:], og, og[:], s1, s1[:], ALU.mult)
            tt("pool", og, og[:], og, og[:], gain, gain[:].unsqueeze(2).broadcast_to([128, 8, 512]), ALU.mult)
        q_, k_, v_ = q4[b4], k4[b4], v4[b4]
        va = vaug[i % 2]
        e_ = ev[i % 2]
        kt = ktm[i % 2]
        hn_ = hn[i % 2]
        cp("pool", va, va[:, :, 0:256], v_, v_[:, tl, :].rearrange("p (h d) -> p h d", h=4))
        tt("pool", e_, e_[:], va, va[:], eT, eT[:, i, :].unsqueeze(2).broadcast_to([128, 4, 257]), ALU.mult)
        for c in range(8):
            tr(psK, psK[:, c, :], k_, k_[:, c, cs], identb, identb[:])
        cp("act", kt, kt[:], psK, psK[:])
        for h in range(4):
            sm_ = sm[n % 2]
            den_ = den[n % 2]
            hs_ = hs[n % 2]
            bst_ = bst[n % 2]
            mv_ = mv[n % 2]
            rs_ = rs[n % 2]
            n += 1
            mm(psS_, psS_[:, 0:128], k_, k_[:, 2 * h, cs], q_, q_[:, 2 * h, cs], start=True, stop=False)
            mm(psS_, psS_[:, 0:128], k_, k_[:, 2 * h + 1, cs], q_, q_[:, 2 * h + 1, cs], start=False, stop=True)
            stt(sm_, sm_[:], psS_, psS_[:, 0:128], eT[:, i, h:h + 1], cmask, cmask[:], ALU.mult, ALU.mult, extra=[eT])
            mm(psX_, psX_[:, 0:257], sm_, sm_[:], va, va[:, h, :], start=True, stop=(i == 0))
            if i > 0:
                mm(psX_, psX_[:, 0:257], q_, q_[:, 2 * h, cs], Cb[h][0], Cb[h][0][:], start=False, stop=False)
                mm(psX_, psX_[:, 0:257], q_, q_[:, 2 * h + 1, cs], Cb[h][1], Cb[h][1][:], start=False, stop=True)
            act(den_, den_[:], psX_, psX_[:, 256:257], AF.Abs)
            tt("dve", den_, den_[:], den_, den_[:], thrT, thrT[:, i, h:h + 1], ALU.max)
            recip(den_, den_[:], den_, den_[:])
            act(hs_, hs_[:], psX_, psX_[:, 0:256], AF.Identity, scale=den_[:], extra=[den_])
            P.op("dve", lambda e, a=bst_, b=hs_: e.bn_stats(out=a[:], in_=b[:]), [hs_], [bst_])
            P.op("dve", lambda e, a=mv_, b=bst_: e.bn_aggr(out=a[:], in_=b[:]), [bst_], [mv_])
            act(rs_, rs_[:], mv_, mv_[:, 1:2], AF.Sqrt, bias=eps5[:], extra=[eps5])
            recip(rs_, rs_[:], rs_, rs_[:])
            ts("dve", hn_, hn_[:, 2 * h:2 * h + 2, :].rearrange("p c d -> p (c d)"), hs_, hs_[:], mv_[:, 0:1], ALU.subtract,
               rs_[:], ALU.mult, extra=[mv_, rs_])
            for c in range(2):
                mm(psC_[c], psC_[c][:, 0:257], kt, kt[:, 2 * h + c, :], e_, e_[:, h, :])
                Dt = Dst[h][c]
                if i == 0:
                    cp("dve", Dt, Dt[:], psC_[c], psC_[c][:, 0:257])
                else:
                    stt(Dt, Dt[:], Dt, Dt[:], decb[:, h, i - 1:i], psC_[c], psC_[c][:, 0:257], ALU.mult, ALU.add, extra=[decb])
                if i < NT - 1:
                    act(Cb[h][c], Cb[h][c][:], Dt, Dt[:], AF.Identity, scale=decb[:, h, i:i + 1], extra=[decb])
        ys = yst[g4 % 2]
        for c in range(8):
            tr(psT, psT[:, c, :], hn_, hn_[:, c, :], identb, identb[:])
        tt("dve", ys, ys[:, :, cs], psT, psT[:], og, og[:, :, cs], ALU.mult)
        if tl == 3 or i == NT - 1:
            t0 = g4 * 512
            ld("sp", fm, fm[0:8, :, t0:t0 + 512].rearrange("c p s -> p c s"), ys, ys[:])
    P.release(mk, tl_)


def phase_conv(P, nc, S, l, C, fm, yT, dww, dwb, lng, lnb, eps5,
               mm, tr, act, tt, ts, stt, cp, memset, ld, recip):
    identb, ones32 = C["identb"], C["ones32"]
    mk = P.mark()
    tl_ = []

    def sb(name, shape, dt):
        t = P.sbuf(name, shape, dt)
        tl_.append(t)
        return t

    SEG = min(S, 1024)
    NB = SEG // 512
    w = sb("dw_w", [128, 8, 31], F32)
    bi = sb("dw_b", [128, 8], F32)
    g = sb("ln_g", [128, 8], F32)
    be = sb("ln_b", [128, 8], F32)
    ld("sp", w, w[:], dww, dww[l])
    ld("sp", bi, bi[:], dwb, dwb[l])
    ld("sp", g, g[:], lng, lng[l])
    ld("sp", be, be[:], lnb, lnb[l])
    diag2 = [sb("diag%d" % c, [128, 31, 128], BF16) for c in range(2)]
    ba = [sb("ba%d" % i, [128, 30 + SEG], BF16) for i in range(2)]
    bb = [sb("bb%d" % i, [128, 30 + SEG], BF16) for i in range(2)]
    glu = [sb("glu%d" % i, [128, 30 + SEG], BF16) for i in range(2)]
    u32 = sb("u32", [128, 8, SEG], F32)
    sq = [sb("usq%d" % i, [128, 512], F32) for i in range(2)]
    su = sb("su", [128, SEG], F32)
    su2 = sb("su2", [128, SEG], F32)
    bz = [sb("bzb%d" % i, [128, SEG], BF16) for i in range(2)]
    z = [sb("zb%d" % i, [128, SEG], F32) for i in range(2)]
    yo = [sb("yo%d" % i, [128, SEG], BF16) for i in range(2)]
    psU = [P.psum("psU%d" % i, [128, 512], F32) for i in range(2)]
    psA = P.psum("psA", [128, 512], F32)
    psB = P.psum("psB", [128, 512], F32)
    cba, cbb, cbz = FMC["ba"], FMC["bb"], FMC["bz"]
    it = 0
    nu = 0
    for sg in range(S // SEG):
        t0 = sg * SEG
        for c in range(8):
            a_, b_, g_ = ba[it % 2], bb[it % 2], glu[it % 2]
            dg = diag2[it % 2]
            it += 1
            for j in range(31):
                ts("pool", dg, dg[:, j, :], identb, identb[:], w[:, c, j:j + 1], ALU.mult, extra=[w])
            if sg == 0:
                memset("pool", a_, a_[:, 0:30], 0.0)
                memset("pool", b_, b_[:, 0:30], 0.0)
                ld("sp", a_, a_[:, 30:], fm, fm[cba + c, :, 0:SEG])
                ld("sp", b_, b_[:, 30:], fm, fm[cbb + c, :, 0:SEG])
            else:
                ld("sp", a_, a_[:], fm, fm[cba + c, :, t0 - 30:t0 + SEG])
                ld("sp", b_, b_[:], fm, fm[cbb + c, :, t0 - 30:t0 + SEG])
            act(b_, b_[:], b_, b_[:], AF.Sigmoid)
            tt("dve", g_, g_[:], a_, a_[:], b_, b_[:], ALU.mult)
            for nb in range(NB):
                ps = psU[nu % 2]
                sq_ = sq[nu % 2]
                nu += 1
                for j in range(31):
                    mm(ps, ps[:], dg, dg[:, j, :], g_, g_[:, nb * 512 + j:nb * 512 + j + 512], start=(j == 0),
                       stop=(j == 30))
                act(u32, u32[:, c, nb * 512:(nb + 1) * 512], ps, ps[:], AF.Identity, bias=bi[:, c:c + 1], extra=[bi])
                act(sq_, sq_[:], u32, u32[:, c, nb * 512:(nb + 1) * 512], AF.Square)
                mm(psA, psA[:], ones32, ones32[:], u32, u32[:, c, nb * 512:(nb + 1) * 512])
                mm(psB, psB[:], ones32, ones32[:], sq_, sq_[:])
                if c == 0:
                    cp("dve", su, su[:, nb * 512:(nb + 1) * 512], psA, psA[:])
                    cp("dve", su2, su2[:, nb * 512:(nb + 1) * 512], psB, psB[:])
                else:
                    tt("dve", su, su[:, nb * 512:(nb + 1) * 512], su, su[:, nb * 512:(nb + 1) * 512], psA, psA[:], ALU.add)
                    tt("dve", su2, su2[:, nb * 512:(nb + 1) * 512], su2, su2[:, nb * 512:(nb + 1) * 512], psB, psB[:],
                       ALU.add)
        ts("dve", su, su[:], su, su[:], 1.0 / D, ALU.mult)
        tt("dve", z[0], z[0][:], su, su[:], su, su[:], ALU.mult)
        stt(su2, su2[:], su2, su2[:], 1.0 / D, z[0], z[0][:], ALU.mult, ALU.subtract)
        act(su2, su2[:], su2, su2[:], AF.Sqrt, bias=eps5[:], extra=[eps5])
        recip(su2, su2[:], su2, su2[:])
        for c in range(8):
            z_ = z[c % 2]
            bz_ = bz[c % 2]
            y_ = yo[c % 2]
            ld("sp", bz_, bz_[:], fm, fm[cbz + c, :, t0:t0 + SEG])
            tt("dve", z_, z_[:], u32, u32[:, c, :], su, su[:], ALU.subtract)
            tt("dve", z_, z_[:], z_, z_[:], su2, su2[:], ALU.mult)
            act(z_, z_[:], z_, z_[:], AF.Identity, bias=be[:, c:c + 1], scale=g[:, c:c + 1], extra=[g, be])
            act(z_, z_[:], z_, z_[:], AF.Silu)
            act(bz_, bz_[:], bz_, bz_[:], AF.Silu)
            tt("pool", y_, y_[:], z_, z_[:], bz_, bz_[:], ALU.mult)
            ld("sp", fm, fm[cbz + c, :, t0:t0 + SEG], y_, y_[:])
    P.release(mk, tl_)


def phase_nsa(P, nc, S, l, C, fm, tm, cgs, yT, pe_c, w1_c, w2_c,
              mm, tr, act, tt, ts, stt, cp, memset, ld, recip):
    NT = S // 128
    NCB = S // 16
    NCC = (NCB + 127) // 128
    identb, onesb = C["identb"], C["onesb"]
    causneg, bandneg, cmpneg, ovl, gmat, bm = C["causneg"], C["bandneg"], C["cmpneg"], C["ovl"], C["gmat"], C["bm"]
    for g in range(2):
        mk = P.mark()
        tl_ = []

        def sb(name, shape, dt):
            t = P.sbuf(name, shape, dt)
            tl_.append(t)
            return t

        kcT = sb("kcT", [128, NCC * 128], BF16)
        VC = sb("VC", [128, NCC, 128], BF16)
        negc = sb("negc", [128, 1], F32)
        stat = sb("stat", [128, 2], F32)
        pT = [sb("pT%d" % i, [128, 512], BF16) for i in range(3)]
        pcT = sb("pcT", [128, NCC, 512], BF16)
        accO = [sb("accO%d" % i, [128, 4, 128], F32) for i in range(2)]
        accB = [sb("accB%d" % i, [128, 4, 128], BF16) for i in range(2)]
        rl = [sb("rl%d" % i, [128, 4], F32) for i in range(3)]
        scl = [sb("scl%d" % i, [128, 4], F32) for i in range(3)]
        imp = [sb("imp%d" % i, [128, 128], F32) for i in range(2)]
        imt = [sb("imt%d" % i, [128, 128], F32) for i in range(2)]
        m8a = [sb("m8a%d" % i, [128, 8], F32) for i in range(2)]
        m8b = [sb("m8b%d" % i, [128, 8], F32) for i in range(2)]
        selb = [sb("selb%d" % i, [128, 128], BF16) for i in range(2)]
        selneg = [sb("selneg%d" % i, [128, 4, 128], BF16) for i in range(2)]
        mk1 = P.mark()
        t1 = []

        def sb1(name, shape, dt):
            t = P.sbuf(name, shape, dt)
            t1.append(t)
            return t

        w1 = sb1("w1", [128, 32, 256], BF16)
        w2 = sb1("w2", [128, 2, 128], BF16)
        pe = sb1("pe", [128, 32], F32)
        xrow = sb1("xrow", [128, S], BF16)
        tmp = [sb1("tmp%d" % i, [128, NCB], BF16) for i in range(4)]
        hx = sb1("hx", [128, NCB], F32)
        hu = sb1("hu", [128, NCB], F32)
        hT = sb1("hT", [128, 2, NCB], BF16)
        psH = [P.psum("psH%d" % i, [128, NCB], F32) for i in range(2)]
        psO = P.psum("psO", [128, 512], F32)
        for j in range(2):
            ld("pool", w1, w1[:], w1_c, w1_c[l, j].rearrange("(p d) h -> d p h", d=128))
            ld("pool", w2, w2[:], w2_c, w2_c[l, j].rearrange("(c h) d -> h c d", h=128))
            ld("sp", pe, pe[:], pe_c, pe_c[l, j])
            src = FMC["ckc"] + g if j == 0 else FMC["cvc"] + g
            for c0 in range(0, S, 2048):
                ld("sp", xrow, xrow[:, c0:c0 + 2048], fm, fm[src, :, c0:c0 + 2048])
            for p_ in range(32):
                t_ = tmp[p_ % 4]
                nval = NCB - 1
                if p_ == 0 and j == 0:
                    for q_ in range(4):
                        memset("pool", tmp[q_], tmp[q_][:, NCB - 1:NCB], 0.0)
                ts("dve", t_, t_[:, 0:nval], xrow, xrow[:, p_:p_ + 16 * (nval - 1) + 1:16], pe[:, p_:p_ + 1], ALU.add,
                   extra=[pe])
                for hc in range(2):
                    mm(psH[hc], psH[hc][:], w1, w1[:, p_, hc * 128:(hc + 1) * 128], t_, t_[:], start=(p_ == 0),
                       stop=(p_ == 31))
            for hc in range(2):
                cp("act", hx, hx[:], psH[hc], psH[hc][:])
                tt("dve", hu, hu[:], hx, hx[:], hx, hx[:], ALU.mult)
                ts("dve", hu, hu[:], hu, hu[:], 0.044715, ALU.mult, 1.0, ALU.add)
                tt("dve", hu, hu[:], hu, hu[:], hx, hx[:], ALU.mult)
                act(hu, hu[:], hu, hu[:], AF.Sigmoid, scale=1.5957691216)
                tt("dve", hT, hT[:, hc, :], hu, hu[:], hx, hx[:], ALU.mult)
            if j == 0:
                for cc in range(NCC):
                    for hc in range(2):
                        mm(psO, psO[:, 0:128], w2, w2[:, hc, :], hT, hT[:, hc, cc * 128:(cc + 1) * 128], start=(hc == 0),
                           stop=(hc == 1))
                    cp("dve", kcT, kcT[:, cc * 128:(cc + 1) * 128], psO, psO[:, 0:128])
            else:
                for cc in range(NCC):
                    for hc in range(2):
                        mm(psO, psO[:, 0:128], hT, hT[:, hc, cc * 128:(cc + 1) * 128], w2, w2[:, hc, :], start=(hc == 0),
                           stop=(hc == 1))
                    cp("dve", VC, VC[:, cc, :], psO, psO[:, 0:128])
        P.release(mk1, t1)

        KsT = sb("KsT", [128, S], BF16)
        KwT = sb("KwT", [128, S], BF16)
        VS = sb("VS", [128, NT, 136], BF16)
        VW = sb("VW", [128, NT, 136], BF16)
        gs = sb("gs", [128, NT, 24], F32)
        for c0 in range(0, S, 2048):
            ld("sp", KsT, KsT[:, c0:c0 + 2048], fm, fm[FMC["cks"] + g, :, c0:c0 + 2048])
            ld("sp", KwT, KwT[:, c0:c0 + 2048], fm, fm[FMC["ckw"] + g, :, c0:c0 + 2048])
        memset("pool", VS, VS[:, :, 128:129], 1.0)
        memset("pool", VW, VW[:, :, 128:129], 1.0)
        for t8 in range(0, NT, 8):
            r0, r1 = t8 * 128, (t8 + 8) * 128
            ld("sp", VS, VS[:, t8:t8 + 8, 0:128], tm,
               tm[r0:r1, 1024 + 128 * g:1024 + 128 * g + 128].rearrange("(t p) c -> p t c", p=128))
            ld("sp", VW, VW[:, t8:t8 + 8, 0:128], tm,
               tm[r0:r1, 1280 + 128 * g:1280 + 128 * g + 128].rearrange("(t p) c -> p t c", p=128))
            ld("sp", gs, gs[:, t8:t8 + 8, :], cgs, cgs[r0:r1, :].rearrange("(t p) c -> p t c", p=128))
        act(gs, gs[:], gs, gs[:], AF.Sigmoid)

        mk2 = P.mark()
        t2 = []
        sq = [P.sbuf("nsq%d" % i, [128, 512], BF16) for i in range(2)]
        qrow = [P.sbuf("qrow%d" % i, [128, 512], BF16) for i in range(2)]
        t2 += sq + qrow
        NBK = S // 512
        stq = P.sbuf("stq", [128, 4 * NBK], F32)
        stk = P.sbuf("stk", [128, 2 * NBK + NCC], F32)
        t2 += [stq, stk]
        psN = [P.psum("psN%d" % i, [128, 512], F32) for i in range(2)]
        nn_ = 0

        def norm_blk(srct, srcap, width, dst, col):
            nonlocal nn_
            s_ = sq[nn_ % 2]
            ps = psN[nn_ % 2]
            nn_ += 1
            act(s_, s_[:, 0:width], srct, srcap, AF.Square)
            mm(ps, ps[:, 0:width], onesb, onesb[:], s_, s_[:, 0:width])
            P.op("dve", lambda e: e.reduce_max(out=dst[:, col:col + 1], in_=ps[:, 0:width], axis=AX.X), [ps], [dst])

        for j in range(4):
            for b in range(NBK):
                qr = qrow[nn_ % 2]
                ld("sp", qr, qr[:], fm, fm[FMC["cq"] + 4 * g + j, :, b * 512:(b + 1) * 512])
                norm_blk(qr, qr[:], 512, stq, j * NBK + b)
        for b in range(NBK):
            norm_blk(KsT, KsT[:, b * 512:(b + 1) * 512], 512, stk, b)
            norm_blk(KwT, KwT[:, b * 512:(b + 1) * 512], 512, stk, NBK + b)
        for cc in range(NCC):
            norm_blk(kcT, kcT[:, cc * 128:(cc + 1) * 128], 128, stk, 2 * NBK + cc)
        P.op("dve", lambda e, a=stat, b=stq: e.reduce_max(out=a[:, 0:1], in_=b[:], axis=AX.X), [stq], [stat])
        P.op("dve", lambda e, a=stat, b=stk: e.reduce_max(out=a[:, 1:2], in_=b[:], axis=AX.X), [stk], [stat])
        tt("dve", negc, negc[:], stat, stat[:, 0:1], stat, stat[:, 1:2], ALU.mult)
        act(negc, negc[:], negc, negc[:], AF.Sqrt)
        ts("dve", negc, negc[:], negc, negc[:], -1.03 * DHS, ALU.mult)
        P.release(mk2, t2)

        q4 = [sb("nq4_%d" % i, [128, 4, 512], BF16) for i in range(2)]
        cz4 = [sb("cz4_%d" % i, [128, 4, 512], BF16) for i in range(2)]
        yst = [sb("nyst%d" % i, [128, 4, 512], BF16) for i in range(2)]
        psST = [P.psum("psST%d" % i, [128, 512], F32) for i in range(2)]
        psOj = [P.psum("psOj%d" % j, [128, 512], F32) for j in range(4)]
        psM = P.psum("psM", [128, 8, 128], BF16)
        nst = 0
        npt = 0
        qc0 = FMC["cq"] + 4 * g
        zc0 = FMC["cz"] + 4 * g
        for i in range(NT):
            g4 = i // 4
            tl = i % 4
            b4 = g4 % 2
            cs = slice(tl * 128, (tl + 1) * 128)
            if tl == 0:
                t0 = g4 * 512
                ld("sp", q4[b4], q4[b4][:], fm, fm[qc0:qc0 + 4, :, t0:t0 + 512].rearrange("c p s -> p c s"))
                ld("sp", cz4[b4], cz4[b4][:], fm, fm[zc0:zc0 + 4, :, t0:t0 + 512].rearrange("c p s -> p c s"))
                act(cz4[b4], cz4[b4][:], cz4[b4], cz4[b4][:], AF.Silu)
            q_ = q4[b4]
            qap = q_[:, :, cs]
            ao = accO[i % 2]
            rl_c, rl_w, rl_s = rl
            sc_c, sc_w, sc_s = scl
            cstar = i // 16
            idx = i % 16
            ncv = min(cstar + 1, NCC)
            for cc in range(ncv):
                ps = psST[nst % 2]
                nst += 1
                mask = None
                if cc == cstar:
                    mask = idx
                elif cc == cstar - 1 and idx == 0:
                    mask = 16
                mm(ps, ps[:], kcT, kcT[:, cc * 128:(cc + 1) * 128], q_, qap, start=True, stop=(mask is None))
                if mask is not None:
                    mm(ps, ps[:], identb, identb[:], cmpneg, cmpneg[:, mask, :, :], start=False, stop=True)
                act(pcT, pcT[:, cc, :], ps, ps[:], AF.Exp, bias=negc[:], scale=DHS, extra=[negc])
            for j in range(4):
                for cc in range(ncv):
                    mm(psOj[j], psOj[j][:, 0:128], pcT, pcT[:, cc, j * 128:(j + 1) * 128], VC, VC[:, cc, :],
                       start=(cc == 0), stop=(cc == ncv - 1))
                for cc in range(ncv):
                    mm(psOj[j], psOj[j][:, 128:256], pcT, pcT[:, cc, j * 128:(j + 1) * 128], ovl, ovl[:, cc, :],
                       start=(cc == 0), stop=(cc == ncv - 1))
            for j in range(4):
                ts("dve", rl_c, rl_c[:, j:j + 1], psOj[j], psOj[j][:, 128:129], 1e-30, ALU.max)
            recip(rl_c, rl_c[:], rl_c, rl_c[:])
            tt("dve", sc_c, sc_c[:], rl_c, rl_c[:], gs, gs[:, i, 12 * g + 0:12 * g + 12:3], ALU.mult)
            im = imp[i % 2]
            it_ = imt[i % 2]
            for j in range(4):
                act(ao, ao[:, j, :], psOj[j], psOj[j][:, 0:128], AF.Identity, scale=sc_c[:, j:j + 1], extra=[sc_c])
                if j == 0:
                    ts("dve", im, im[:], psOj[j], psOj[j][:, 128:256], rl_c[:, 0:1], ALU.mult, extra=[rl_c])
                else:
                    stt(im, im[:], psOj[j], psOj[j][:, 128:256], rl_c[:, j:j + 1], im, im[:], ALU.mult, ALU.add,
                        extra=[rl_c])
            tt("dve", im, im[:], im, im[:], bm, bm[:, 127 - 2 * i:127 - 2 * i + 128], ALU.add)
            memset("dve", im, im[:, 0:1], 1e4)
            ma, mb = m8a[i % 2], m8b[i % 2]
            P.op("dve", lambda e, a=ma, b=im: e.max(out=a[:], in_=b[:]), [im], [ma])
            P.op("dve", lambda e, a=it_, b=ma, c=im: e.match_replace(out=a[:], in_to_replace=b[:], in_values=c[:],
                                                                    imm_value=-3e4), [ma, im], [it_])
            P.op("dve", lambda e, a=mb, b=it_: e.max(out=a[:], in_=b[:]), [it_], [mb])
            sb_ = selb[i % 2]
            ts("dve", it_, it_[:], im, im[:], mb[:, 7:8], ALU.is_ge, extra=[mb])
            ts("dve", sb_, sb_[:], it_, it_[:], -NEGM, ALU.mult, NEGM, ALU.add)
            tr(psM, psM[:, 0, :], sb_, sb_[:], identb, identb[:])
            sn = selneg[i % 2]
            cp("act", sn, sn[:], psM, psM[:, 0:1, :].broadcast_to([128, 4, 128]))
            kcs = [kc for kc in range(i - 4, i + 1) if kc >= 0]
            for kc in kcs:
                ps = psST[nst % 2]
                nst += 1
                p_ = pT[npt % 3]
                npt += 1
                mask = causneg if kc == i else (bandneg if kc == i - 4 else None)
                mm(ps, ps[:], KwT, KwT[:, kc * 128:(kc + 1) * 128], q_, qap, start=True, stop=(mask is None))
                if mask is not None:
                    mm(ps, ps[:], identb, identb[:], mask, mask[:], start=False, stop=True)
                act(p_, p_[:], ps, ps[:], AF.Exp, bias=negc[:], scale=DHS, extra=[negc])
                for j in range(4):
                    mm(psOj[j], psOj[j][:, 0:129], p_, p_[:, j * 128:(j + 1) * 128], VW, VW[:, kc, 0:129], start=(kc == kcs[0]),
                       stop=(kc == i))
            for j in range(4):
                ts("dve", rl_w, rl_w[:, j:j + 1], psOj[j], psOj[j][:, 128:129], 1e-30, ALU.max)
            recip(rl_w, rl_w[:], rl_w, rl_w[:])
            tt("dve", sc_w, sc_w[:], rl_w, rl_w[:], gs, gs[:, i, 12 * g + 2:12 * g + 12:3], ALU.mult)
            for j in range(4):
                stt(ao, ao[:, j, :], psOj[j], psOj[j][:, 0:128], sc_w[:, j:j + 1], ao, ao[:, j, :], ALU.mult, ALU.add,
                    extra=[sc_w])
            for kc in range(i + 1):
                ps = psST[nst % 2]
                nst += 1
                p_ = pT[npt % 3]
                npt += 1
                mm(ps, ps[:], KsT, KsT[:, kc * 128:(kc + 1) * 128], q_, qap, start=True, stop=False)
                mm(ps, ps[:], gmat, gmat[:, kc * 128:(kc + 1) * 128], sn, sn[:], start=False, stop=(kc != i))
                if kc == i:
                    mm(ps, ps[:], identb, identb[:], causneg, causneg[:], start=False, stop=True)
                act(p_, p_[:], ps, ps[:], AF.Exp, bias=negc[:], scale=DHS, extra=[negc])
                for j in range(4):
                    mm(psOj[j], psOj[j][:, 0:129], p_, p_[:, j * 128:(j + 1) * 128], VS, VS[:, kc, 0:129], start=(kc == 0),
                       stop=(kc == i))
            for j in range(4):
                ts("dve", rl_s, rl_s[:, j:j + 1], psOj[j], psOj[j][:, 128:129], 1e-30, ALU.max)
            recip(rl_s, rl_s[:], rl_s, rl_s[:])
            tt("dve", sc_s, sc_s[:], rl_s, rl_s[:], gs, gs[:, i, 12 * g + 1:12 * g + 12:3], ALU.mult)
            ab = accB[i % 2]
            for j in range(4):
                stt(ab, ab[:, j, :], psOj[j], psOj[j][:, 0:128], sc_s[:, j:j + 1], ao, ao[:, j, :], ALU.mult, ALU.add,
                    extra=[sc_s])
            for j in range(4):
                tr(psM, psM[:, j, :], ab, ab[:, j, :], identb, identb[:])
            ys = yst[g4 % 2]
            tt("dve", ys, ys[:, :, cs], psM, psM[:, 0:4, :], cz4[b4], cz4[b4][:, :, cs], ALU.mult)
            if tl == 3 or i == NT - 1:
                t0 = g4 * 512
                ld("sp", fm, fm[zc0:zc0 + 4, :, t0:t0 + 512].rearrange("c p s -> p c s"), ys, ys[:])
        P.release(mk, tl_)


def phase_tail(P, nc, S, l, C, fm, yT, w_br, w_o, x_src, x_dst, Gada, eps6,
               mm, tr, act, tt, ts, stt, cp, memset, ld, recip):
    ones32 = C["ones32"]
    mk = P.mark()
    tl_ = []

    def sb(name, shape, dt):
        t = P.sbuf(name, shape, dt)
        tl_.append(t)
        return t

    wb = [sb("wb%d" % i, [128, 8, 1024], BF16) for i in range(3)]
    wo = sb("wo", [128, 8, 1024], BF16)
    for br in range(3):
        ld("pool", wb[br], wb[br][:], w_br, w_br[l, br].rearrange("(k p) n -> p k n", p=128))
    ld("pool", wo, wo[:], w_o, w_o[l].rearrange("(k p) n -> p k n", p=128))
    TT = 256
    yb = [sb("yb%d" % i, [128, 24, TT], BF16) for i in range(1)] * 2
    mg = [sb("mgb%d" % i, [128, 24, TT], BF16) for i in range(1)] * 2
    tq = [sb("tq%d" % i, [128, TT], F32) for i in range(3)]
    mT = sb("mT", [128, 8, TT], BF16)
    o32 = sb("o32", [128, 8, TT], F32)
    sq = [sb("tsq%d" % i, [128, TT], F32) for i in range(2)]
    rstd = sb("trstd", [128, TT], F32)
    xb = [sb("txb%d" % i, [128, 8, TT], F32) for i in range(2)]
    xn = [sb("txn%d" % i, [128, 8, TT], F32) for i in range(2)]
    psP = [P.psum("psP%d" % i, [128, 512], F32) for i in range(3)]
    psQ = [P.psum("psQ%d" % i, [128, 512], F32) for i in range(2)]
    psR = P.psum("psR", [128, 512], F32)
    m0 = FMC["mg"]
    for tb in range(S // TT):
        t0 = tb * TT
        y_ = yb[tb % 2]
        g_ = mg[tb % 2]
        x_ = xb[tb % 2]
        xn_ = xn[tb % 2]
        for br_, c0_ in enumerate((FMC["aq"], FMC["bz"], FMC["cz"])):
            ld("sp", y_, y_[:, br_ * 8:(br_ + 1) * 8, :], fm, fm[c0_:c0_ + 8, :, t0:t0 + TT].rearrange("c p s -> p c s"))
        for br_ in range(3):
            ld("sp", g_, g_[:, br_ * 8:(br_ + 1) * 8, :], fm,
               fm[m0 + 8 * br_:m0 + 8 * br_ + 8, :, t0:t0 + TT].rearrange("c p s -> p c s"))
        ld("sp", x_, x_[:], x_src, x_src[:, :, t0:t0 + TT].rearrange("k p s -> p k s"))
        act(g_, g_[:], g_, g_[:], AF.Sigmoid)
        for dc in range(8):
            for br in range(3):
                for k in range(8):
                    mm(psP[br], psP[br][:, 0:TT], wb[br], wb[br][:, k, dc * 128:(dc + 1) * 128], y_, y_[:, br * 8 + k, :],
                       start=(k == 0), stop=(k == 7))
                tt("dve", tq[br], tq[br][:], psP[br], psP[br][:, 0:TT], g_, g_[:, br * 8 + dc, :], ALU.mult)
            tt("pool", tq[0], tq[0][:], tq[0], tq[0][:], tq[1], tq[1][:], ALU.add)
            tt("pool", mT, mT[:, dc, :], tq[0], tq[0][:], tq[2], tq[2][:], ALU.add)
        for dc in range(8):
            ps = psQ[dc % 2]
            s_ = sq[dc % 2]
            for k in range(8):
                mm(ps, ps[:, 0:TT], wo, wo[:, k, dc * 128:(dc + 1) * 128], mT, mT[:, k, :], start=(k == 0), stop=(k == 7))
            cp("act", o32, o32[:, dc, :], ps, ps[:, 0:TT])
            act(s_, s_[:], o32, o32[:, dc, :], AF.Square)
            mm(psR, psR[:, 0:TT], ones32, ones32[:], s_, s_[:], start=(dc == 0), stop=(dc == 7))
        act(rstd, rstd[:], psR, psR[:, 0:TT], AF.Sqrt, bias=eps6[:], scale=1.0 / D, extra=[eps6])
        recip(rstd, rstd[:], rstd, rstd[:])
        tt("dve", o32, o32[:], o32, o32[:], rstd, rstd[:].unsqueeze(1).broadcast_to([128, 8, TT]), ALU.mult)
        for dc in range(8):
            stt(xn_, xn_[:, dc, :], o32, o32[:, dc, :], Gada[:, dc:dc + 1], x_, x_[:, dc, :], ALU.mult, ALU.add,
                extra=[Gada])
        ld("sp", x_dst, x_dst[:, :, t0:t0 + TT].rearrange("k p s -> p k s"), xn_, xn_[:])
    P.release(mk, tl_)


def prep_inputs(inp, S, L):
    f = np.float32
    bf = ml_dtypes.bfloat16
    sh = {}
    w_in = np.asarray(inp["w_in"], f)[:L]
    b_in = np.asarray(inp["b_in"], f)[:L]
    sh["w_ada"] = np.ascontiguousarray(np.asarray(inp["w_ada"], f)[:L])
    sh["b_ada"] = np.ascontiguousarray(np.asarray(inp["b_ada"], f)[:L].reshape(L, 24, 128).transpose(0, 2, 1))
    sh["g_pre"] = np.ascontiguousarray(np.asarray(inp["norm_pre"], f)[:L].reshape(L, 8, 128).transpose(0, 2, 1))
    sh["g_post"] = np.ascontiguousarray(np.asarray(inp["norm_post"], f)[:L].reshape(L, 8, 128).transpose(0, 2, 1))
    sh["w_in"] = np.ascontiguousarray(w_in[:, :, PERM])
    bp = b_in[:, PERM]
    sh["b_fm"] = np.ascontiguousarray(bp[:, :NFMC].reshape(L, NFM, 128).transpose(0, 2, 1))
    sh["b_tm"] = np.ascontiguousarray(np.concatenate([bp[:, NFMC:NFMC + NTM], bp[:, NFMC + NTM + 8:]], axis=1)[:, None, :])
    sh["b_sm"] = np.ascontiguousarray(bp[:, NFMC + NTM:NFMC + NTM + 8][:, :, None])
    cw = np.asarray(inp["mlstm_conv_w"], f)[:L]
    sh["cvw_a"] = np.ascontiguousarray(cw.reshape(L, 4, 16, 128).transpose(0, 3, 2, 1))
    sh["cvb_a"] = np.ascontiguousarray(np.asarray(inp["mlstm_conv_b"], f)[:L].reshape(L, 16, 128).transpose(0, 2, 1))
    sh["gn_a"] = np.ascontiguousarray(np.asarray(inp["mlstm_norm"], f)[:L].reshape(L, 8, 128).transpose(0, 2, 1))
    dw = np.asarray(inp["conf_dw_w"], f)[:L]
    sh["dww"] = np.ascontiguousarray(dw.reshape(L, 31, 8, 128).transpose(0, 3, 2, 1))
    sh["dwb"] = np.ascontiguousarray(np.asarray(inp["conf_dw_b"], f)[:L].reshape(L, 8, 128).transpose(0, 2, 1))
    sh["lng"] = np.ascontiguousarray(np.asarray(inp["conf_ln_g"], f)[:L].reshape(L, 8, 128).transpose(0, 2, 1))
    sh["lnb"] = np.ascontiguousarray(np.asarray(inp["conf_ln_b"], f)[:L].reshape(L, 8, 128).transpose(0, 2, 1))
    sh["pe_c"] = np.ascontiguousarray(np.asarray(inp["nsa_cmp_pe"], f)[:L].transpose(0, 1, 3, 2))
    sh["w1_c"] = np.ascontiguousarray(np.asarray(inp["nsa_cmp_w1"], f)[:L])
    sh["w2_c"] = np.ascontiguousarray(np.asarray(inp["nsa_cmp_w2"], f)[:L])
    sh["w_br"] = np.ascontiguousarray(np.asarray(inp["w_branch"], f)[:L])
    sh["w_o"] = np.ascontiguousarray(np.asarray(inp["w_out"], f)[:L])
    for k, v in make_consts(S).items():
        sh["c_" + k] = v
    x = np.asarray(inp["x"], f)
    c = np.asarray(inp["c"], f)
    maps = []
    for core in range(8):
        b = core // 2
        m = dict(sh)
        m["xT"] = np.ascontiguousarray(x[b, :S].T.reshape(8, 128, S))
        ct = c[b].reshape(8, 128).T
        m["cT"] = np.ascontiguousarray(np.stack([ct, ct], axis=1))
        maps.append(m)
    return maps


_NC_CACHE = {}


def run(inp, S, L, dbg=False, phases="01ABCT"):
    key = (S, L, dbg, phases)
    import time
    t0 = time.time()
    if key not in _NC_CACHE:
        _NC_CACHE[key] = build(S, L, dbg, phases)
    nc = _NC_CACHE[key]
    print("build time", time.time() - t0, flush=True)
    t0 = time.time()
    maps = prep_inputs(inp, S, L)
    print("prep time", time.time() - t0, flush=True)
    res = run_bass_kernel_spmd(nc, maps, core_ids=list(range(8)))
    return res.results


def kernel(**inputs):
    S = 8192
    L = 4
    res = run(inputs, S, L)
    out = np.empty((4, S, D), np.float32)
    for b in range(4):
        oT = np.asarray(res[2 * b]["outT"], np.float32).reshape(1024, S)
        out[b] = oT.T
    return out
```
